# Optimizing a Trainium2 kernel written in Bass

```python
import jax
import jax.numpy as jnp
from jax import lax
import numpy as np

D_MODEL = 2048
BATCH = 2
SEQ = 4096
DEPTH = 4

N_MIXERS = 3
CHUNK = 128
EPS = 1e-6

RET_HEADS = 8
RET_DK = D_MODEL // RET_HEADS
RET_DV = 2 * D_MODEL // RET_HEADS
RET_IN = 2 * RET_HEADS * RET_DK + 2 * RET_HEADS * RET_DV
ROPE_BASE = 10000.0

GMLP_DFFN = 6 * D_MODEL
GMLP_HALF = GMLP_DFFN // 2
GMLP_GROUPS = 8
GMLP_GW = GMLP_HALF // GMLP_GROUPS

FOX_HEADS = 16
FOX_HD = D_MODEL // FOX_HEADS
FOX_IN = 4 * FOX_HEADS * FOX_HD + FOX_HEADS

FFN_HIDDEN = -(-8 * D_MODEL // (3 * 256)) * 256

N_RET = (DEPTH + N_MIXERS - 1) // N_MIXERS
N_GMLP = (DEPTH + N_MIXERS - 2) // N_MIXERS
N_FOX = DEPTH // N_MIXERS

kernel_name = "hybrid_ret_gmlp_fox_trunk"


def rms_norm(x, g):
    xf = x.astype(jnp.float32)
    y = xf * lax.rsqrt(jnp.mean(xf * xf, axis=-1, keepdims=True) + EPS)
    return (y * g.astype(jnp.float32)).astype(x.dtype)


def standardize(x):
    xf = x.astype(jnp.float32)
    xc = xf - jnp.mean(xf, axis=-1, keepdims=True)
    return xc * lax.rsqrt(jnp.mean(xc * xc, axis=-1, keepdims=True) + EPS)


def rotary(t, positions):
    half = t.shape[-1] // 2
    inv_freq = ROPE_BASE ** (-jnp.arange(half, dtype=jnp.float32) / half)
    ang = positions.astype(jnp.float32)[:, :, None, None] * inv_freq
    cos, sin = jnp.cos(ang), jnp.sin(ang)
    t1, t2 = t[..., :half], t[..., half:]
    return jnp.concatenate([t1 * cos - t2 * sin, t2 * cos + t1 * sin], axis=-1)


def retention_mixer(h, positions, w_in, gn_g, w_out):
    f32 = jnp.float32
    B, S, _ = h.shape
    H, dk, dv, C = RET_HEADS, RET_DK, RET_DV, CHUNK
    nc = S // C
    proj = h @ w_in
    q, k, v, g = jnp.split(proj, [H * dk, 2 * H * dk, 2 * H * dk + H * dv], axis=-1)
    q = rotary(q.astype(f32).reshape(B, S, H, dk), positions)
    k = rotary(k.astype(f32).reshape(B, S, H, dk), positions) * (dk ** -0.5)
    v = v.astype(f32).reshape(B, S, H, dv)

    def to_chunks(t):
        return t.reshape(B, nc, C, H, t.shape[-1]).transpose(1, 0, 3, 2, 4)

    log_gamma = jnp.log1p(-jnp.exp2(-5.0 - jnp.arange(H, dtype=f32)))
    idx = jnp.arange(C, dtype=f32)
    dist = idx[:, None] - idx[None, :]
    decay_in = jnp.where(dist >= 0,
                         jnp.exp(jnp.maximum(dist, 0.0)[None] * log_gamma[:, None, None]),
                         0.0)
    q_decay = jnp.exp((idx + 1.0)[None, :] * log_gamma[:, None])[..., None]
    k_decay = jnp.exp((C - 1.0 - idx)[None, :] * log_gamma[:, None])[..., None]
    chunk_decay = jnp.exp(C * log_gamma)[:, None, None]

    def step(state, xs):
        qi, ki, vi = xs
        scores = jnp.einsum('bhnd,bhmd->bhnm', qi, ki) * decay_in
        inner = jnp.einsum('bhnm,bhme->bhne', scores, vi)
        cross = jnp.einsum('bhnd,bhde->bhne', qi, state) * q_decay
        state = state * chunk_decay + jnp.einsum('bhmd,bhme->bhde', ki * k_decay, vi)
        return state, inner + cross

    state0 = jnp.zeros((B, H, dk, dv), f32)
    _, o = lax.scan(step, state0, (to_chunks(q), to_chunks(k), to_chunks(v)))
    o = o.transpose(1, 0, 3, 2, 4).reshape(B, S, H, dv)
    o = standardize(o).reshape(B, S, H * dv) * gn_g.astype(f32)
    y = jax.nn.silu(g.astype(f32)) * o
    return y.astype(h.dtype) @ w_out


def gmlp_mixer(h, w_in, ln_g, ln_b, w_s, b_s, w_out):
    f32 = jnp.float32
    B, S, _ = h.shape
    G, gw, C = GMLP_GROUPS, GMLP_GW, CHUNK
    nc = S // C
    z = jax.nn.gelu(h @ w_in, approximate=False)
    u, v = jnp.split(z, 2, axis=-1)
    v = (standardize(v) * ln_g.astype(f32) + ln_b.astype(f32)).astype(h.dtype)
    v = v.reshape(B, nc, C, G, gw)
    w_causal = jnp.tril(w_s)
    mixed = jnp.einsum('gts,bnsgc->bntgc', w_causal, v) + b_s.T[None, None, :, :, None]
    y = u * mixed.reshape(B, S, GMLP_HALF)
    return y @ w_out


def fox_mixer(h, w_in, b_f, qn_g, kn_g, w_out):
    f32 = jnp.float32
    B, S, _ = h.shape
    H, hd, C = FOX_HEADS, FOX_HD, CHUNK
    D = H * hd
    nb = S // C
    proj = h @ w_in
    q, k, v, g, f_logit = jnp.split(proj, [D, 2 * D, 3 * D, 4 * D], axis=-1)
    q = rms_norm(q.reshape(B, S, H, hd), qn_g).astype(f32)
    k = rms_norm(k.reshape(B, S, H, hd), kn_g).astype(f32)
    v = v.reshape(B, S, H, hd).astype(f32)
    log_f = jax.nn.log_sigmoid(f_logit.astype(f32) + b_f.astype(f32))
    c = jnp.cumsum(log_f, axis=1)
    scale = hd ** -0.5
    q_blocks = q.reshape(B, nb, C, H, hd).transpose(1, 0, 3, 2, 4)
    c_blocks = c.reshape(B, nb, C, H).transpose(1, 0, 3, 2)
    k_all = k.transpose(0, 2, 1, 3)
    v_all = v.transpose(0, 2, 1, 3)
    c_keys = c.transpose(0, 2, 1)
    key_pos = jnp.arange(S)

    def attend_block(args):
        qi, ci, bi = args
        q_pos = bi * C + jnp.arange(C)
        logits = (jnp.einsum('bhqd,bhkd->bhqk', qi, k_all) * scale
                  + (ci[..., :, None] - c_keys[:, :, None, :]))
        logits = jnp.where(key_pos[None, :] <= q_pos[:, None], logits, -jnp.inf)
        p = jax.nn.softmax(logits, axis=-1)
        return jnp.einsum('bhqk,bhkd->bhqd', p, v_all)

    o = lax.map(attend_block, (q_blocks, c_blocks, jnp.arange(nb)))
    o = o.transpose(1, 0, 3, 2, 4).reshape(B, S, D)
    y = jax.nn.sigmoid(g.astype(f32)) * o
    return y.astype(h.dtype) @ w_out


def swiglu(h, w_gate, w_up, w_down):
    return (jax.nn.silu(h @ w_gate) * (h @ w_up)) @ w_down


def _normal(key, shape, scale):
    return jax.random.normal(key, shape, jnp.float32) * scale


def setup_inputs(seed: int = 0) -> dict:
    key = jax.random.key(seed)
    ks = jax.random.split(key, 24)
    D = D_MODEL
    out_scale = (2.0 * DEPTH) ** -0.5
    return {
        "x": _normal(ks[0], (BATCH, SEQ, D), 1.0),
        "positions": jnp.broadcast_to(jnp.arange(SEQ, dtype=jnp.int32), (BATCH, SEQ)),
        "mix_norm_g": 1.0 + _normal(ks[1], (DEPTH, D), 0.1),
        "ffn_norm_g": 1.0 + _normal(ks[2], (DEPTH, D), 0.1),
        "ret_w_in": _normal(ks[3], (N_RET, D, RET_IN), D ** -0.5),
        "ret_gn_g": 1.0 + _normal(ks[4], (N_RET, RET_HEADS * RET_DV), 0.1),
        "ret_w_out": _normal(ks[5], (N_RET, RET_HEADS * RET_DV, D), (RET_HEADS * RET_DV) ** -0.5 * out_scale),
        "gmlp_w_in": _normal(ks[6], (N_GMLP, D, GMLP_DFFN), D ** -0.5),
        "gmlp_ln_g": 1.0 + _normal(ks[7], (N_GMLP, GMLP_HALF), 0.1),
        "gmlp_ln_b": _normal(ks[8], (N_GMLP, GMLP_HALF), 0.1),
        "gmlp_w_s": _normal(ks[9], (N_GMLP, GMLP_GROUPS, CHUNK, CHUNK), 0.5 * CHUNK ** -0.5),
        "gmlp_b_s": 1.0 + _normal(ks[10], (N_GMLP, GMLP_GROUPS, CHUNK), 0.1),
        "gmlp_w_out": _normal(ks[11], (N_GMLP, GMLP_HALF, D), GMLP_HALF ** -0.5 * out_scale),
        "fox_w_in": _normal(ks[12], (N_FOX, D, FOX_IN), D ** -0.5),
        "fox_b_f": jax.random.uniform(ks[13], (N_FOX, FOX_HEADS), jnp.float32, 1.0, 5.0),
        "fox_qn_g": 1.0 + _normal(ks[14], (N_FOX, FOX_HD), 0.1),
        "fox_kn_g": 1.0 + _normal(ks[15], (N_FOX, FOX_HD), 0.1),
        "fox_w_out": _normal(ks[16], (N_FOX, D, D), D ** -0.5 * out_scale),
        "ffn_w_gate": _normal(ks[17], (DEPTH, D, FFN_HIDDEN), D ** -0.5),
        "ffn_w_up": _normal(ks[18], (DEPTH, D, FFN_HIDDEN), D ** -0.5),
        "ffn_w_down": _normal(ks[19], (DEPTH, FFN_HIDDEN, D), FFN_HIDDEN ** -0.5 * out_scale),
    }


def reference(x, positions, mix_norm_g, ffn_norm_g,
              ret_w_in, ret_gn_g, ret_w_out,
              gmlp_w_in, gmlp_ln_g, gmlp_ln_b, gmlp_w_s, gmlp_b_s, gmlp_w_out,
              fox_w_in, fox_b_f, fox_qn_g, fox_kn_g, fox_w_out,
              ffn_w_gate, ffn_w_up, ffn_w_down):
    for i in range(DEPTH):
        kind, j = i % N_MIXERS, i // N_MIXERS
        h = rms_norm(x, mix_norm_g[i])
        if kind == 0:
            y = retention_mixer(h, positions, ret_w_in[j], ret_gn_g[j], ret_w_out[j])
        elif kind == 1:
            y = gmlp_mixer(h, gmlp_w_in[j], gmlp_ln_g[j], gmlp_ln_b[j],
                           gmlp_w_s[j], gmlp_b_s[j], gmlp_w_out[j])
        else:
            y = fox_mixer(h, fox_w_in[j], fox_b_f[j], fox_qn_g[j], fox_kn_g[j], fox_w_out[j])
        x = x + y
        h = rms_norm(x, ffn_norm_g[i])
        x = x + swiglu(h, ffn_w_gate[i], ffn_w_up[i], ffn_w_down[i])
    return x
```

```python
import numpy as np
import concourse.bass as bass
import concourse.mybir as mybir
from concourse.bass_utils import run_bass_kernel_spmd

F32 = mybir.dt.float32
BF16 = mybir.dt.bfloat16
I32 = mybir.dt.int32
AF = mybir.ActivationFunctionType
ALU = mybir.AluOpType
AX = mybir.AxisListType

D = 2048
SEQ = 4096
NB = 2
T = 1024
KC = D // 128
FFN_H = 5632
EPS = 1e-6
NCORES = 8
ENGS = ("pe", "act", "dve", "pool", "sp")
SAME_ENG_WAIT = True


class Dummy:
    def __getitem__(self, k):
        return self

    def __getattr__(self, k):
        return lambda *a, **kw: self


class Op:
    __slots__ = ("eng", "fn", "deps", "needs_inc", "idx", "tok", "pos")

    def __init__(self, eng, fn):
        self.eng = eng
        self.fn = fn
        self.deps = []
        self.needs_inc = False
        self.idx = 0
        self.tok = None


class P:
    def __init__(self, nc, dry=False):
        self.nc = nc
        self.dry = dry
        self.q = {e: [] for e in ENGS}
        self.last_w = {}
        self.readers = {}
        self.dsem_cnt = {}
        self.nalloc = 0

    def sb(self, name, shape, dtype):
        if self.dry:
            return Dummy()
        return self.nc.alloc_sbuf_tensor("sb_" + name, list(shape), dtype)

    def psum(self, name):
        if self.dry:
            return Dummy()
        return self.nc.alloc_psum_tensor("psum_" + name, [128, 512], F32)

    def rec(self, eng, fn, r=(), w=(), dma=None):
        if self.dry:
            return None
        op = Op(eng, fn)
        deps = []
        for k in r:
            lw = self.last_w.get(k)
            if lw is not None:
                deps.append(lw)
        for k in w:
            lw = self.last_w.get(k)
            if lw is not None:
                deps.append(lw)
            deps.extend(self.readers.get(k, ()))
        best = {}
        for d in deps:
            if d.tok is None:
                if d.eng == eng and (eng == "pe" or not SAME_ENG_WAIT):
                    continue
                k = ("e", d.eng)
                if k not in best or best[k].pos < d.pos:
                    best[k] = d
            else:
                k = ("d", d.tok[0])
                if k not in best or best[k].tok[1] < d.tok[1]:
                    best[k] = d
        for d in best.values():
            if d.tok is None:
                d.needs_inc = True
            op.deps.append(d)
        for k in r:
            self.readers.setdefault(k, []).append(op)
        for k in w:
            self.last_w[k] = op
            self.readers[k] = []
        if dma is not None:
            c = self.dsem_cnt.get(dma, 0) + 16
            self.dsem_cnt[dma] = c
            op.tok = (dma, c)
        op.pos = len(self.q[eng])
        self.q[eng].append(op)
        return op

    def dma(self, eng, out, in_, r=(), w=(), sem=None):
        assert sem is not None
        return self.rec(eng, lambda e: e.dma_start(out=out, in_=in_), r, w, dma=sem)

    def mm(self, out, pairs, r=(), w=(), start=True, stop=True):
        def fn(e):
            n = len(pairs)
            ins = None
            for i, (l, rr) in enumerate(pairs):
                ins = e.matmul(out, l, rr, start=(start and i == 0), stop=(stop and i == n - 1))
            return ins
        return self.rec("pe", fn, r, w)

    def act(self, out, in_, func, r=(), w=(), **kw):
        return self.rec("act", lambda e: e.activation(out=out, in_=in_, func=func, **kw), r, w)

    def tt(self, eng, out, in0, in1, op, r=(), w=()):
        return self.rec(eng, lambda e: e.tensor_tensor(out=out, in0=in0, in1=in1, op=op), r, w)

    def ts(self, eng, out, in0, s1, s2, op0, op1=None, r=(), w=()):
        if op1 is None:
            return self.rec(eng, lambda e: e.tensor_scalar(out, in0, s1, None, op0), r, w)
        return self.rec(eng, lambda e: e.tensor_scalar(out, in0, s1, s2, op0, op1), r, w)

    def stt(self, eng, out, in0, scalar, in1, op0, op1, r=(), w=()):
        return self.rec(eng, lambda e: e.scalar_tensor_tensor(out=out, in0=in0, scalar=scalar, in1=in1,
                                                              op0=op0, op1=op1), r, w)

    def copy(self, eng, out, in_, r=(), w=()):
        if eng == "act":
            return self.rec(eng, lambda e: e.activation(out=out, in_=in_, func=AF.Copy), r, w)
        return self.rec(eng, lambda e: e.tensor_copy(out=out, in_=in_), r, w)

    def memset(self, eng, ap, val, w=()):
        return self.rec(eng, lambda e: e.memset(ap, val), (), w)

    def emit(self, final_wait_keys=()):
        nc = self.nc
        fin = self.rec("sp", None, r=final_wait_keys)
        for e in ENGS:
            c = 0
            for op in self.q[e]:
                if op.needs_inc:
                    c += 1
                    op.idx = c
        esem = {e: nc.alloc_semaphore(f"sem_{e}") for e in ENGS}
        dsem = {k: nc.alloc_semaphore(f"dsem_{i}") for i, k in enumerate(self.dsem_cnt)}
        engmap = {"pe": "tensor", "act": "scalar", "dve": "vector", "pool": "gpsimd", "sp": "sync"}
        self.stats = {e: len(self.q[e]) for e in ENGS}

        def run(e, eng):
            waited = {}
            for op in self.q[e]:
                need = {}
                for d in op.deps:
                    if d.tok is not None:
                        s, v = dsem[d.tok[0]], d.tok[1]
                        key = ("d", d.tok[0])
                    else:
                        s, v = esem[d.eng], d.idx
                        key = ("e", d.eng)
                    if v > need.get(key, (None, 0))[1]:
                        need[key] = (s, v)
                for key, (s, v) in need.items():
                    if waited.get(key, 0) >= v:
                        continue
                    waited[key] = v
                    eng.wait_ge(s, v)
                if op.fn is None:
                    continue
                ins = op.fn(eng)
                if op.tok is not None:
                    ins.then_inc(dsem[op.tok[0]], 16)
                elif op.needs_inc:
                    ins.then_inc(esem[e], 1)

        with nc.Block() as block:
            for e in ENGS:
                getattr(block, engmap[e])(lambda eng, e=e: run(e, eng))


class WStream:
    SLOT = 8192

    def __init__(self, p, nslot, plan=None):
        self.p = p
        self.nslot = nslot
        self.planning = plan is None
        self.plan = [] if plan is None else plan
        self.issued = 0
        self.consumed = 0
        self.free = [True] * nslot
        self.slots = [p.sb(f"wslot{i}", [128, self.SLOT], BF16) for i in range(nslot)]

    def _issue(self):
        while self.issued < len(self.plan) and self.free[self.issued % self.nslot]:
            i = self.issued
            s = i % self.nslot
            src, shape = self.plan[i]
            n = int(np.prod(shape))
            dst = self.slots[s][:, 0:n]
            if len(shape) == 2:
                dst = dst.rearrange("p (a b) -> p a b", a=shape[0])
            self.p.dma("pool", dst, src, r=(), w=[("w", s)], sem=("w", s))
            self.free[s] = False
            self.issued += 1

    def get(self, src, shape):
        if self.planning:
            self.plan.append((src, shape))
            return Dummy(), None, None
        if self.consumed == 0:
            self._issue()
        i = self.consumed
        self.consumed += 1
        s = i % self.nslot
        assert i < self.issued, "weight stream starvation (slot never released?)"
        n = int(np.prod(shape))
        v = self.slots[s][:, 0:n]
        if len(shape) == 2:
            v = v.rearrange("p (a b) -> p a b", a=shape[0])
        return v, ("w", s), s

    def done(self, h):
        if self.planning:
            return
        self.free[h] = True
        self._issue()


def colpanel(W, c0, n):
    return W[:, c0:c0 + n].rearrange("(kc p) n -> p kc n", p=128), (W.shape[0] // 128, n)


def rowpanel(W, r0, nch, c0=0, n=None):
    n = W.shape[1] - c0 if n is None else n
    return W[r0:r0 + nch * 128, c0:c0 + n].rearrange("(kc p) n -> p kc n", p=128), (nch, n)


class Ctx:
    pass


def setup_common(p, prm_dram):
    c = Ctx()
    c.p = p
    c.xT = p.sb("xT", [128, KC, T], F32)
    c.hT = p.sb("hT", [128, KC, T], BF16)
    c.ones = p.sb("ones", [128, 128], BF16)
    c.prm = p.sb("prm", [128, PRM_COLS], F32)
    c.rstd = p.sb("rstd", [128, T], F32)
    c.ps = [p.psum(f"ps{i}") for i in range(8)]
    c.psi = 0
    p.memset("dve", c.ones[:], 1.0, w=["ones"])
    p.dma("sp", c.prm[:], prm_dram, w=["prm"], sem="prm")
    return c


def ffn_alloc(c):
    p = c.p
    c.actb = [p.sb(f"actb{i}", [128, 4, T], BF16) for i in range(2)]
    c.sg = [p.sb(f"sg{i}", [128, 512], F32) for i in range(2)]
    c.sgi = 0


def next_ps(c):
    i = c.psi
    c.psi = (i + 1) % 8
    return c.ps[i], ("ps", i)


PRM_MIXG = 0
PRM_FFNG = 64
PRM_COLS = 128


def rmsnorm(c, gcol):
    p = c.p
    for kc in range(KC):
        if kc % 2 == 0:
            p.act(c.hT[:, kc, :], c.xT[:, kc, :], AF.Square, r=[("xT", kc)], w=[("hT", kc)])
        else:
            p.tt("pool", c.hT[:, kc, :], c.xT[:, kc, :], c.xT[:, kc, :], ALU.mult, r=[("xT", kc)], w=[("hT", kc)])
    for half in range(2):
        ps, pk = next_ps(c)
        p.mm(ps[:], [(c.ones[:], c.hT[:, kc, half * 512:(half + 1) * 512]) for kc in range(KC)],
             r=["ones"] + [("hT", kc) for kc in range(KC)], w=[pk])
        sl = slice(half * 512, (half + 1) * 512)
        p.ts("dve", c.rstd[:, sl], ps[:], 1.0 / D, EPS, ALU.mult, ALU.add, r=[pk], w=[("rstd", half)])
        p.act(c.rstd[:, sl], c.rstd[:, sl], AF.Sqrt, r=[("rstd", half)], w=[("rstd", half)])
        p.rec("dve", lambda e, sl=sl: e.reciprocal(out=c.rstd[:, sl], in_=c.rstd[:, sl]),
              r=[("rstd", half)], w=[("rstd", half)])
    for kc in range(KC):
        p.stt("dve", c.hT[:, kc, :], c.xT[:, kc, :], c.prm[:, gcol + kc:gcol + kc + 1], c.rstd[:],
              ALU.mult, ALU.mult, r=[("xT", kc), "prm", ("rstd", 0), ("rstd", 1)], w=[("hT", kc)])


def ffn(c, ws, L, Wg, Wu, Wd):
    p = c.p
    rmsnorm(c, PRM_FFNG + 16 * L)
    hkeys = [("hT", kc) for kc in range(KC)]
    NG = FFN_H // 512
    for gi in range(NG):
        wg, wgk, wgh = ws.get(*colpanel(Wg, gi * 512, 512))
        wu, wuk, wuh = ws.get(*colpanel(Wu, gi * 512, 512))
        a = c.actb[gi % 2]
        ak = ("actb", gi % 2)
        for ch in range(4):
            for half in range(2):
                sl = slice(half * 512, (half + 1) * 512)
                pg, pgk = next_ps(c)
                p.mm(pg[:], [(wg[:, kc, ch * 128:(ch + 1) * 128], c.hT[:, kc, sl]) for kc in range(KC)],
                     r=[wgk] + hkeys, w=[pgk])
                pu, puk = next_ps(c)
                p.mm(pu[:], [(wu[:, kc, ch * 128:(ch + 1) * 128], c.hT[:, kc, sl]) for kc in range(KC)],
                     r=[wuk] + hkeys, w=[puk])
                sg = c.sg[c.sgi % 2]
                sgk = ("sg", c.sgi % 2)
                c.sgi += 1
                p.act(sg[:], pg[:], AF.Silu, r=[pgk], w=[sgk])
                p.tt("dve", a[:, ch, sl], sg[:], pu[:], ALU.mult, r=[sgk, puk], w=[(ak, ch, half)])
        ws.done(wgh)
        ws.done(wuh)
        wd, wdk, wdh = ws.get(*rowpanel(Wd, gi * 512, 4))
        for oc in range(KC):
            for half in range(2):
                sl = slice(half * 512, (half + 1) * 512)
                po, pok = next_ps(c)
                p.mm(po[:], [(wd[:, ch, oc * 128:(oc + 1) * 128], a[:, ch, sl]) for ch in range(4)],
                     r=[wdk] + [(ak, ch, half) for ch in range(4)], w=[pok])
                p.tt("dve", c.xT[:, oc, sl], c.xT[:, oc, sl], po[:], ALU.add, r=[pok, ("xT", oc)], w=[("xT", oc)])
        ws.done(wdh)


def load_xT(c, x_dram):
    p = c.p
    xv = x_dram.rearrange("(kc p) t -> p kc t", p=128)
    for kc in range(KC):
        p.dma("sp" if kc % 2 == 0 else "act", c.xT[:, kc, :], xv[:, kc, :], w=[("xT", kc)], sem=("xld", kc % 4))


def store_xT(c, y_dram):
    p = c.p
    yv = y_dram.rearrange("(kc p) t -> p kc t", p=128)
    keys = []
    for kc in range(KC):
        k = ("yst", kc)
        p.dma("sp", yv[:, kc, :], c.xT[:, kc, :], r=[("xT", kc)], w=[k], sem=("xst", kc % 4))
        keys.append(k)
    return keys


RET_H = 8
PRM_INVF = 128
PRM_GN = 129
PRM_COEF = 193
PRM_KDEC = 217
PRM_COLS = 281
TWO_PI = float(2 * np.pi)


def ret_tables_load(c, cs_dram):
    p = c.p
    c.cos = p.sb("cos", [128, T], F32)
    c.sin = p.sb("sin", [128, T], F32)
    p.dma("sp", c.cos[:], cs_dram[0], w=["cos"], sem="cosld")
    p.dma("act", c.sin[:], cs_dram[1], w=["sin"], sem="sinld")


def ret_tables(c, pos_dram, cs_dram=None):
    p = c.p
    c.cos = p.sb("cos", [128, T], F32)
    c.sin = p.sb("sin", [128, T], F32)
    posi = p.sb("posi", [128, T], I32)
    ang = p.sb("ang", [128, T], F32)
    p.dma("sp", posi[:], pos_dram.partition_broadcast(128), w=["posi"], sem="posi")
    S2PI = float(2 * np.pi * (1 - 2e-7))
    p.copy("dve", ang[:], posi[:], r=["posi"], w=["ang"])
    p.ts("dve", ang[:], ang[:], c.prm[:, PRM_INVF:PRM_INVF + 1], None, ALU.mult, r=["ang", "prm"], w=["ang"])
    p.copy("dve", posi[:], ang[:], r=["ang"], w=["posi"])
    p.copy("dve", c.sin[:], posi[:], r=["posi"], w=["sin"])
    p.tt("dve", ang[:], ang[:], c.sin[:], ALU.subtract, r=["ang", "sin"], w=["ang"])
    p.stt("dve", c.sin[:], ang[:], 0.5, ang[:], ALU.is_gt, ALU.subtract, r=["ang"], w=["sin"])
    p.stt("dve", c.sin[:], ang[:], -0.5, c.sin[:], ALU.is_lt, ALU.subtract, r=["ang", "sin"], w=["sin"])
    p.act(c.sin[:], c.sin[:], AF.Sin, r=["sin"], w=["sin"], scale=S2PI)
    p.ts("dve", ang[:], ang[:], 0.25, None, ALU.add, r=["ang"], w=["ang"])
    p.stt("dve", c.cos[:], ang[:], 0.5, ang[:], ALU.is_gt, ALU.subtract, r=["ang"], w=["cos"])
    p.stt("dve", c.cos[:], ang[:], -0.5, c.cos[:], ALU.is_lt, ALU.subtract, r=["ang", "cos"], w=["cos"])
    p.act(c.cos[:], c.cos[:], AF.Sin, r=["cos"], w=["cos"], scale=S2PI)
    if cs_dram is not None:
        p.dma("sp", cs_dram[0], c.cos[:], r=["cos"], w=["cs_d0"], sem="cosst")
        p.dma("sp", cs_dram[1], c.sin[:], r=["sin"], w=["cs_d1"], sem="sinst")


def rotary_from_psum(c, ps1, k1, ps2, k2, out1, out2, sl, rkeys, wkeys):
    p = c.p
    ta, tb = c.rt[0], c.rt[1]
    p.tt("dve", ta[:], ps1[:], c.cos[:, sl], ALU.mult, r=[k1, "cos"], w=["rta"])
    p.tt("dve", tb[:], ps2[:], c.sin[:, sl], ALU.mult, r=[k2, "sin"], w=["rtb"])
    p.tt("dve", out1, ta[:], tb[:], ALU.subtract, r=["rta", "rtb"] + rkeys, w=[wkeys[0]])
    p.tt("dve", ta[:], ps2[:], c.cos[:, sl], ALU.mult, r=[k2, "cos"], w=["rta"])
    p.tt("dve", tb[:], ps1[:], c.sin[:, sl], ALU.mult, r=[k1, "sin"], w=["rtb"])
    p.tt("dve", out2, ta[:], tb[:], ALU.add, r=["rta", "rtb"] + rkeys, w=[wkeys[1]])


def ret_alloc(c):
    p = c.p
    c.rt = [p.sb(f"rt{i}", [128, 512], F32) for i in range(2)]
    c.kT = p.sb("kT", [128, 2, T], BF16)
    c.vtm = p.sb("vtm", [128, 8, 512], BF16)
    c.ident = p.sb("ident", [128, 128], BF16)
    c.cpi = p.sb("cpi", [128, 1], F32)
    p.memset("dve", c.cpi[:], float(np.pi), w=["cpi"])


def ret_phase_a(c, ws, L, j, Win, pos_dram, cst_ident, kT_s, v_s, S_loc):
    p = c.p
    rmsnorm(c, PRM_MIXG + 16 * L)
    hkeys = [("hT", kc) for kc in range(KC)]
    kdtm = p.sb("kdtm", [128, 8, 256], BF16)
    sst = p.sb("sst", [128, 2, 512], F32)
    p.dma("pool", c.ident[:], cst_ident, w=["ident"], sem="ident")
    for h in range(RET_H):
        wk, wkk, wkh = ws.get(*colpanel(Win, 2048 + h * 256, 256))
        for half in range(2):
            sl = slice(half * 512, (half + 1) * 512)
            p1, k1 = next_ps(c)
            p.mm(p1[:], [(wk[:, kc, 0:128], c.hT[:, kc, sl]) for kc in range(KC)], r=[wkk] + hkeys, w=[k1])
            p2, k2 = next_ps(c)
            p.mm(p2[:], [(wk[:, kc, 128:256], c.hT[:, kc, sl]) for kc in range(KC)], r=[wkk] + hkeys, w=[k2])
            rotary_from_psum(c, p1, k1, p2, k2, c.kT[:, 0, sl], c.kT[:, 1, sl], sl, [], [("kT", 0, half), ("kT", 1, half)])
        ws.done(wkh)
        kTk = [("kT", a, b) for a in range(2) for b in range(2)]
        p.dma("sp", kT_s[:, 2 * h:2 * h + 2, :], c.kT[:], r=kTk, w=[("kT_s", h)], sem="kTst")
        wv, wvk, wvh = ws.get(*colpanel(Win, 4096 + h * 512, 512))
        for tc in range(8):
            pv, pvk = next_ps(c)
            p.mm(pv[:], [(c.hT[:, kc, tc * 128:(tc + 1) * 128], wv[:, kc, :]) for kc in range(KC)],
                 r=[wvk] + hkeys, w=[pvk])
            p.copy("act", c.vtm[:, tc, :], pv[:], r=[pvk], w=[("vtm", tc)])
        ws.done(wvh)
        vk = [("vtm", tc) for tc in range(8)]
        p.dma("sp", v_s[:, :, h * 512:(h + 1) * 512], c.vtm[:], r=vk, w=[("v_s", h)], sem="vst")
        for tc in range(8):
            pt, ptk = next_ps(c)
            for a in range(2):
                p.mm(pt[:, a * 128:(a + 1) * 128], [(c.kT[:, a, tc * 128:(tc + 1) * 128], c.ident[:])],
                     r=kTk + ["ident"], w=[ptk])
            col = PRM_KDEC + tc * 8 + h
            p.act(kdtm[:, tc, :], pt[:, 0:256], AF.Copy, r=[ptk, "prm"], w=[("kdtm", tc)],
                  scale=c.prm[:, col:col + 1])
        for dc in range(2):
            pS, pSk = next_ps(c)
            p.mm(pS[:], [(kdtm[:, tc, dc * 128:(dc + 1) * 128], c.vtm[:, tc, :]) for tc in range(8)],
                 r=[("kdtm", tc) for tc in range(8)] + vk, w=[pSk])
            p.copy("act", sst[:, dc, :], pS[:], r=[pSk], w=[("sst", dc)])
        p.dma("sp", S_loc[h].rearrange("dc p e -> p dc e"), sst[:], r=[("sst", 0), ("sst", 1)], w=[("S_loc", h)], sem="sst")
    return [("kT_s", h) for h in range(8)] + [("v_s", h) for h in range(8)] + [("S_loc", h) for h in range(8)]


NW = 256


def ret_phase_b(c, ws, L, j, Win, Wout, cst_wtab, cst_qdec, kT_s, v_s, S_all):
    p = c.p
    hkeys = [("hT", kc) for kc in range(KC)]
    qT = p.sb("qT", [128, 2, T], BF16)
    qdT = p.sb("qdT", [128, 2, T], BF16)
    wtab = p.sb("wtab", [128, T], F32)
    qdec = p.sb("qdec", [128, T], F32)
    PT = p.sb("PT", [128, 8, NW], BF16)
    Sb = p.sb("Sb", [128, 3, 2, 512], BF16)
    oT = p.sb("oT", [128, 4, NW], F32)
    obf = p.sb("obf", [128, 4, NW], BF16)
    osq = p.sb("osq", [128, 4, NW], BF16)
    mean = p.sb("mean", [128, NW], F32)
    var = p.sb("var", [128, NW], F32)
    msq = p.sb("msq", [128, NW], F32)
    sgt = p.sb("sgt", [128, NW], F32)
    yT = p.sb("yT", [128, 4, NW], BF16)
    kTk = [("kT", a, b) for a in range(2) for b in range(2)]
    vk = [("vtm", tc) for tc in range(8)]
    for h in range(RET_H):
        p.dma("sp", c.kT[:], kT_s[:, 2 * h:2 * h + 2, :], w=kTk, sem="kTld")
        p.dma("act", c.vtm[:], v_s[:, :, h * 512:(h + 1) * 512], w=vk, sem="vld")
        p.dma("sp", wtab[:], cst_wtab[h], w=["wtab"], sem="wtab")
        p.dma("act", qdec[:], cst_qdec[h], w=["qdec"], sem="qdec")
        for s_ in range(3):
            p.dma("pool", Sb[:, s_], S_all[s_, h].rearrange("dc p e -> p dc e"), w=["Sb"], sem="Sb")
        for s in range(3):
            col = PRM_COEF + h * 3 + s
            p.act(Sb[:, s], Sb[:, s], AF.Copy, r=["Sb", "prm"], w=["Sb"], scale=c.prm[:, col:col + 1])
        wq, wqk, wqh = ws.get(*colpanel(Win, h * 256, 256))
        for half in range(2):
            sl = slice(half * 512, (half + 1) * 512)
            p1, k1 = next_ps(c)
            p.mm(p1[:], [(wq[:, kc, 0:128], c.hT[:, kc, sl]) for kc in range(KC)], r=[wqk] + hkeys, w=[k1])
            p2, k2 = next_ps(c)
            p.mm(p2[:], [(wq[:, kc, 128:256], c.hT[:, kc, sl]) for kc in range(KC)], r=[wqk] + hkeys, w=[k2])
            rotary_from_psum(c, p1, k1, p2, k2, qT[:, 0, sl], qT[:, 1, sl], sl, [], [("qT", 0, half), ("qT", 1, half)])
            for a in range(2):
                p.tt("pool", qdT[:, a, sl], qT[:, a, sl], qdec[:, sl], ALU.mult, r=[("qT", a, half), "qdec"],
                     w=[("qdT", a, half)])
        ws.done(wqh)
        wg, wgk, wgh = ws.get(*colpanel(Win, 8192 + h * 512, 512))
        wo, wok, woh = ws.get(*rowpanel(Wout, h * 512, 4))
        for win in range(T // NW):
            n0 = win * NW
            sl = slice(n0, n0 + NW)
            nmc = (n0 + NW) // 128
            half = n0 // 512
            for mc in range(nmc):
                lo = max(mc * 128, n0)
                ps, psk = next_ps(c)
                w_ = n0 + NW - lo
                p.mm(ps[:, 0:w_], [(c.kT[:, a, mc * 128:(mc + 1) * 128], qT[:, a, lo:n0 + NW]) for a in range(2)],
                     r=kTk + [("qT", a, half) for a in range(2)], w=[psk])
                p.tt("dve", PT[:, mc, lo - n0:NW], ps[:, 0:w_], wtab[:, lo - mc * 128:n0 + NW - mc * 128], ALU.mult,
                     r=[psk, "wtab"], w=[("PT", mc)])
            for ec in range(4):
                po, pok = next_ps(c)
                esl = slice(ec * 128, (ec + 1) * 128)
                pairs = [(Sb[:, s, a, esl], qdT[:, a, sl]) for s in range(3) for a in range(2)]
                p.mm(po[:, 0:NW], pairs, r=["Sb"] + [("qdT", a, half) for a in range(2)], w=[pok], stop=False)
                for mc in range(nmc):
                    lo = max(mc * 128, n0)
                    p.mm(po[:, lo - n0:NW], [(c.vtm[:, mc, esl], PT[:, mc, lo - n0:NW])],
                         r=vk + [("PT", mc)], w=[pok], start=False, stop=(mc == nmc - 1))
                p.copy("act", oT[:, ec, :], po[:, 0:NW], r=[pok], w=[("oT", ec)])
                p.copy("pool", obf[:, ec, :], oT[:, ec, :], r=[("oT", ec)], w=[("obf", ec)])
                p.tt("pool", osq[:, ec, :], oT[:, ec, :], oT[:, ec, :], ALU.mult, r=[("oT", ec)], w=[("osq", ec)])
            pm, pmk = next_ps(c)
            p.mm(pm[:, 0:NW], [(c.ones[:], obf[:, ec, :]) for ec in range(4)], r=["ones"] + [("obf", ec) for ec in range(4)], w=[pmk])
            pq, pqk = next_ps(c)
            p.mm(pq[:, 0:NW], [(c.ones[:], osq[:, ec, :]) for ec in range(4)], r=["ones"] + [("osq", ec) for ec in range(4)], w=[pqk])
            p.ts("dve", mean[:], pm[:, 0:NW], 1.0 / 512, None, ALU.mult, r=[pmk], w=["mean"])
            p.tt("dve", msq[:], mean[:], mean[:], ALU.mult, r=["mean"], w=["msq"])
            p.stt("dve", var[:], pq[:, 0:NW], 1.0 / 512, msq[:], ALU.mult, ALU.subtract, r=[pqk, "msq"], w=["var"])
            p.ts("dve", var[:], var[:], 0.0, EPS, ALU.max, ALU.add, r=["var"], w=["var"])
            p.act(var[:], var[:], AF.Sqrt, r=["var"], w=["var"])
            p.rec("dve", lambda e: e.reciprocal(out=var[:], in_=var[:]), r=["var"], w=["var"])
            for ec in range(4):
                pg, pgk = next_ps(c)
                p.mm(pg[:, 0:NW], [(wg[:, kc, ec * 128:(ec + 1) * 128], c.hT[:, kc, sl]) for kc in range(KC)], r=[wgk] + hkeys, w=[pgk])
                p.act(sgt[:], pg[:, 0:NW], AF.Silu, r=[pgk], w=["sgt"])
                p.tt("pool", oT[:, ec, :], oT[:, ec, :], mean[:], ALU.subtract, r=[("oT", ec), "mean"], w=[("oT", ec)])
                p.tt("pool", oT[:, ec, :], oT[:, ec, :], var[:], ALU.mult, r=[("oT", ec), "var"], w=[("oT", ec)])
                col = PRM_GN + 32 * j + h * 4 + ec
                p.stt("dve", yT[:, ec, :], oT[:, ec, :], c.prm[:, col:col + 1], sgt[:], ALU.mult, ALU.mult,
                      r=[("oT", ec), "prm", "sgt"], w=[("yT", ec)])
            for oc in range(KC):
                po, pok = next_ps(c)
                p.mm(po[:, 0:NW], [(wo[:, ec, oc * 128:(oc + 1) * 128], yT[:, ec, :]) for ec in range(4)],
                     r=[wok] + [("yT", ec) for ec in range(4)], w=[pok])
                p.tt("dve", c.xT[:, oc, sl], c.xT[:, oc, sl], po[:, 0:NW], ALU.add, r=[pok, ("xT", oc)], w=[("xT", oc)])
        ws.done(wgh)
        ws.done(woh)


def ret_gammas():
    return (1.0 - np.exp2(-5.0 - np.arange(RET_H, dtype=np.float64)))


def make_consts():
    g = ret_gammas()
    cst = {}
    cst["ident"] = np.eye(128, dtype=np.float32)
    m = np.arange(128)[:, None].astype(np.float64)
    jn = np.arange(1024)[None, :].astype(np.float64)
    wt = np.zeros((RET_H, 128, 1024), np.float32)
    qd = np.zeros((RET_H, 128, 1024), np.float32)
    for h in range(RET_H):
        w = np.where(jn >= m, g[h] ** np.maximum(jn - m, 0.0), 0.0) / 16.0
        wt[h] = w.astype(np.float32)
        qd[h] = np.broadcast_to((g[h] ** (jn + 1.0)).astype(np.float32), (128, 1024))
    cst["wtab"] = wt
    cst["qdec"] = qd
    return cst


def make_prm(inp, core, L0=None, j0=0):
    seg = core % 4
    prm = np.zeros((128, PRM_COLS), np.float32)
    for L in range(4):
        Ls = L if L0 is None else L0
        prm[:, PRM_MIXG + 16 * L:PRM_MIXG + 16 * L + 16] = np.asarray(inp["mix_norm_g"][Ls]).reshape(16, 128).T
        prm[:, PRM_FFNG + 16 * L:PRM_FFNG + 16 * L + 16] = np.asarray(inp["ffn_norm_g"][Ls]).reshape(16, 128).T
    half = 128
    invf = (np.float32(10000.0) ** (-(np.arange(half, dtype=np.float32) / np.float32(half)))).astype(np.float32)
    prm[:, PRM_INVF] = (invf.astype(np.float64) / (2 * np.pi)).astype(np.float32)
    for j in range(2):
        js = j if L0 is None else j0
        prm[:, PRM_GN + 32 * j:PRM_GN + 32 * j + 32] = np.asarray(inp["ret_gn_g"][js]).reshape(32, 128).T
    if "fox_qn_g" in inp:
        prm[:, PRM_QNG] = np.asarray(inp["fox_qn_g"][0])
        prm[:, PRM_KNG] = np.asarray(inp["fox_kn_g"][0])
    for r in range(3):
        prm[:, PRM_FXNEG + r] = 0.0 if seg - 3 + r >= 0 else NEG
    g = ret_gammas()
    for h in range(RET_H):
        for s in range(3):
            prm[:, PRM_COEF + h * 3 + s] = (g[h] ** (1024.0 * (seg - 1 - s))) if s < seg else 0.0
        for tc in range(8):
            mm_ = tc * 128 + np.arange(128)
            prm[:, PRM_KDEC + tc * 8 + h] = (g[h] ** (1023.0 - mm_)) / 16.0
    return prm


GM_HALF = 6144
GM_G = 8
GM_GW = 768


def gmlp_layer(c, ws, L, Win, Wout, lng_d, lnb_d, wsT_d, bs_d, mask_d, vraw):
    p = c.p
    rmsnorm(c, PRM_MIXG + 16 * L)
    hkeys = [("hT", kc) for kc in range(KC)]
    vst = [p.sb(f"vst{i}", [128, 512], F32) for i in range(2)]
    bst = p.sb("bst", [128, 8, 12, 6], F32)
    mv = p.sb("mv", [128, 8, 2], F32)
    wcf = [p.sb(f"wcf{i}", [128, 128], F32) for i in range(2)]
    mk = p.sb("mk", [128, 128], F32)
    wcb = p.sb("wcb", [128, 8, 128], BF16)
    bsr = p.sb("bsr", [128, 8, 128], F32)
    p.dma("sp", mk[:], mask_d, w=["mk"], sem="mk")
    p.dma("sp", bsr[:].rearrange("p g t -> p (g t)"), bs_d.partition_broadcast(128), w=["bsr"], sem="bsr")
    for g in range(GM_G):
        p.dma("sp", wcf[g % 2][:], wsT_d[g], w=[("wcf", g % 2)], sem=("wcf", g % 2))
        p.tt("pool", wcb[:, g, :], wcf[g % 2][:], mk[:], ALU.mult, r=[("wcf", g % 2), "mk"], w=["wcb"])
    vi = 0
    for pi in range(12):
        wv, wvk, wvh = ws.get(*colpanel(Win, GM_HALF + pi * 512, 512))
        for tc in range(8):
            ps, psk = next_ps(c)
            p.mm(ps[:], [(c.hT[:, kc, tc * 128:(tc + 1) * 128], wv[:, kc, :]) for kc in range(KC)], r=[wvk] + hkeys, w=[psk])
            v_ = vst[vi % 2]
            vk_ = ("vst", vi % 2)
            vi += 1
            p.act(v_[:], ps[:], AF.Gelu, r=[psk], w=[vk_])
            p.rec("dve", lambda e, v_=v_, tc=tc, pi=pi: e.bn_stats(out=bst[:, tc, pi, :], in_=v_[:]), r=[vk_], w=[("bst", tc, pi)])
            p.dma("sp", vraw[tc, :, pi * 512:(pi + 1) * 512], v_[:], r=[vk_], w=[("vraw", tc, pi)], sem=("vrst", vi % 2))
        ws.done(wvh)
    for tc in range(8):
        p.rec("dve", lambda e, tc=tc: e.bn_aggr(out=mv[:, tc, :], in_=bst[:, tc].rearrange("p a b -> p (a b)")),
              r=[("bst", tc, pi) for pi in range(12)], w=[("mv", tc)])
        p.ts("dve", mv[:, tc, 1:2], mv[:, tc, 1:2], EPS, None, ALU.add, r=[("mv", tc)], w=[("mv", tc)])
        p.act(mv[:, tc, 1:2], mv[:, tc, 1:2], AF.Sqrt, r=[("mv", tc)], w=[("mv", tc)])
        p.rec("dve", lambda e, tc=tc: e.reciprocal(out=mv[:, tc, 1:2], in_=mv[:, tc, 1:2]), r=[("mv", tc)], w=[("mv", tc)])
    vgp = [p.sb(f"vgp{i}", [128, GM_GW], F32) for i in range(2)]
    vn = p.sb("vn", [128, 8, GM_GW], BF16)
    lng = p.sb("lng", [128, GM_GW], F32)
    lnb = p.sb("lnb", [128, GM_GW], F32)
    uT = p.sb("uT", [128, T], F32)
    tmp = p.sb("gtmp", [128, 512], F32)
    yT = p.sb("gyT", [128, 6, T], BF16)
    li = 0
    for g in range(GM_G):
        gsl = slice(g * GM_GW, (g + 1) * GM_GW)
        p.dma("act", lng[:], lng_d[:, gsl].partition_broadcast(128), w=["lng"], sem="lng")
        p.dma("act", lnb[:], lnb_d[:, gsl].partition_broadcast(128), w=["lnb"], sem="lnb")
        for tc in range(8):
            vg = vgp[li % 2]
            vgk = ("vgp", li % 2)
            li += 1
            p.dma("sp", vg[:], vraw[tc, :, gsl], r=[("vraw", tc, pi) for pi in range(12)], w=[vgk], sem=("vgld", li % 2))
            p.ts("dve", vg[:], vg[:], mv[:, tc, 0:1], mv[:, tc, 1:2], ALU.subtract, ALU.mult, r=[vgk, ("mv", tc)], w=[vgk])
            p.tt("pool", vg[:], vg[:], lng[:], ALU.mult, r=[vgk, "lng"], w=[vgk])
            p.tt("pool", vn[:, tc, :], vg[:], lnb[:], ALU.add, r=[vgk, "lnb"], w=[("vn", tc)])
        wo_ = [ws_ for ws_ in ()]
        for h3 in range(2):
            wu, wuk, wuh = ws.get(*colpanel(Win, g * GM_GW + h3 * 384, 384))
            for c3 in range(3):
                cc = h3 * 3 + c3
                for half in range(2):
                    sl = slice(half * 512, (half + 1) * 512)
                    pu, puk = next_ps(c)
                    p.mm(pu[:], [(wu[:, kc, c3 * 128:(c3 + 1) * 128], c.hT[:, kc, sl]) for kc in range(KC)], r=[wuk] + hkeys, w=[puk])
                    p.act(uT[:, sl], pu[:], AF.Gelu, r=[puk], w=[("uT", half)])
                    pm, pmk = next_ps(c)
                    for tl in range(4):
                        tc = half * 4 + tl
                        p.mm(pm[:, tl * 128:(tl + 1) * 128], [(vn[:, tc, cc * 128:(cc + 1) * 128], wcb[:, g, :])],
                             r=[("vn", tc), "wcb"], w=[pmk])
                    for tl in range(4):
                        p.tt("dve", tmp[:, tl * 128:(tl + 1) * 128], pm[:, tl * 128:(tl + 1) * 128], bsr[:, g, :], ALU.add,
                             r=[pmk, "bsr"], w=["gtmp"])
                    p.tt("dve", yT[:, cc, sl], tmp[:], uT[:, sl], ALU.mult, r=["gtmp", ("uT", half)], w=[("gyT", cc, half)])
            ws.done(wuh)
        wa, wak, wah = ws.get(*rowpanel(Wout, g * GM_GW, 3))
        wb, wbk, wbh = ws.get(*rowpanel(Wout, g * GM_GW + 384, 3))
        for oc in range(KC):
            for half in range(2):
                sl = slice(half * 512, (half + 1) * 512)
                po, pok = next_ps(c)
                pairs = [(wa[:, cc, oc * 128:(oc + 1) * 128], yT[:, cc, sl]) for cc in range(3)] + \
                        [(wb[:, cc, oc * 128:(oc + 1) * 128], yT[:, 3 + cc, sl]) for cc in range(3)]
                p.mm(po[:], pairs, r=[wak, wbk] + [("gyT", cc, half) for cc in range(6)], w=[pok])
                p.tt("dve", c.xT[:, oc, sl], c.xT[:, oc, sl], po[:], ALU.add, r=[pok, ("xT", oc)], w=[("xT", oc)])
        ws.done(wah)
        ws.done(wbh)


FX_H = 16
NEG = -30000.0
PRM_QNG = 281
PRM_KNG = 282
PRM_FXNEG = 283
PRM_COLS = 286


def fox_qk(c, ws, Win, col0, gcol, h, wpanel, dst, dkeys):
    p = c.p
    hkeys = [("hT", kc) for kc in range(KC)]
    w_, wk_ = wpanel
    hl = h % 4
    for half in range(2):
        sl = slice(half * 512, (half + 1) * 512)
        ps, psk = next_ps(c)
        p.mm(ps[:], [(w_[:, kc, hl * 128:(hl + 1) * 128], c.hT[:, kc, sl]) for kc in range(KC)], r=[wk_] + hkeys, w=[psk])
        p.copy("act", c.fraw[:], ps[:], r=[psk], w=["fraw"])
        p.act(c.fsq[:], ps[:], AF.Square, r=[psk], w=["fsq"])
        p2, p2k = next_ps(c)
        p.mm(p2[:], [(c.ones[:], c.fsq[:])], r=["ones", "fsq"], w=[p2k])
        p.ts("dve", c.frs[:], p2[:], 1.0 / 128, EPS, ALU.mult, ALU.add, r=[p2k], w=["frs"])
        p.act(c.frs[:], c.frs[:], AF.Sqrt, r=["frs"], w=["frs"])
        p.rec("dve", lambda e: e.reciprocal(out=c.frs[:], in_=c.frs[:]), r=["frs"], w=["frs"])
        p.stt("dve", dst[:, sl], c.fraw[:], c.prm[:, gcol:gcol + 1], c.frs[:], ALU.mult, ALU.mult,
              r=["fraw", "frs", "prm"], w=[dkeys[half]])


def fox_alloc(c):
    p = c.p
    c.fraw = p.sb("fraw", [128, 512], F32)
    c.fsq = p.sb("fsq", [128, 512], BF16)
    c.frs = p.sb("frs", [128, 512], F32)


def fox_phase_a(c, ws, L, Win, bf_d, tri_d, KT_d, V_d, cl_d):
    p = c.p
    rmsnorm(c, PRM_MIXG + 16 * L)
    hkeys = [("hT", kc) for kc in range(KC)]
    kt = [p.sb(f"fkt{i}", [128, T], BF16) for i in range(2)]
    vt = [p.sb(f"fvt{i}", [128, 512], BF16) for i in range(2)]
    lf = p.sb("lf", [128, 8, 16], F32)
    bfr = p.sb("bfr", [128, 16], F32)
    tri = p.sb("tri", [128, 128], F32)
    onef = p.sb("onef", [128, 128], F32)
    cl = p.sb("cl", [128, 8, 16], F32)
    p.dma("sp", bfr[:], bf_d.partition_broadcast(128), w=["bfr"], sem="bfr")
    p.dma("sp", tri[:], tri_d, w=["tri"], sem="tri")
    p.memset("pool", onef[:], 1.0, w=["onef"])
    outk = []
    for h in range(FX_H):
        if h % 4 == 0:
            wk, wkk, wkh = ws.get(*colpanel(Win, 2048 + h * 128, 512))
        t_ = kt[h % 2]
        dk = [("fkt", h % 2, 0), ("fkt", h % 2, 1)]
        fox_qk(c, ws, Win, 2048, PRM_KNG, h, (wk, wkk), t_, dk)
        p.dma("sp", KT_d[h], t_[:], r=dk, w=[("KT_d", h)], sem=("ktst", h % 2))
        outk.append(("KT_d", h))
        if h % 4 == 3:
            ws.done(wkh)
    vi = 0
    for cb in range(4):
        wv, wvk, wvh = ws.get(*colpanel(Win, 4096 + cb * 512, 512))
        for tc in range(8):
            ps, psk = next_ps(c)
            p.mm(ps[:], [(c.hT[:, kc, tc * 128:(tc + 1) * 128], wv[:, kc, :]) for kc in range(KC)], r=[wvk] + hkeys, w=[psk])
            v_ = vt[vi % 2]
            vk_ = ("fvt", vi % 2)
            p.copy("act", v_[:], ps[:], r=[psk], w=[vk_])
            p.dma("sp", V_d[tc, :, cb * 512:(cb + 1) * 512], v_[:], r=[vk_], w=[("V_d", tc, cb)], sem=("vtst", vi % 2))
            outk.append(("V_d", tc, cb))
            vi += 1
        ws.done(wvh)
    wf, wfk, wfh = ws.get(*colpanel(Win, 8192, 16))
    for tc in range(8):
        ps, psk = next_ps(c)
        p.mm(ps[:, 0:16], [(c.hT[:, kc, tc * 128:(tc + 1) * 128], wf[:, kc, :]) for kc in range(KC)], r=[wfk] + hkeys, w=[psk])
        p.tt("dve", lf[:, tc, :], ps[:, 0:16], bfr[:], ALU.add, r=[psk, "bfr"], w=[("lf", tc)])
    ws.done(wfh)
    lfk = [("lf", tc) for tc in range(8)]
    p.act(lf[:], lf[:], AF.Exp, r=lfk, w=lfk, scale=-1.0)
    p.ts("dve", lf[:], lf[:], 1.0, None, ALU.add, r=lfk, w=lfk)
    p.act(lf[:], lf[:], AF.Ln, r=lfk, w=lfk)
    p.ts("dve", lf[:], lf[:], -1.0, None, ALU.mult, r=lfk, w=lfk)
    for tc in range(8):
        ps, psk = next_ps(c)
        pairs = [(onef[:], lf[:, t2, :]) for t2 in range(tc)] + [(tri[:], lf[:, tc, :])]
        p.mm(ps[:, 0:16], pairs, r=lfk + ["onef", "tri"], w=[psk])
        p.copy("act", cl[:, tc, :], ps[:, 0:16], r=[psk], w=[("cl", tc)])
    p.dma("sp", cl_d.rearrange("t p h -> p t h"), cl[:], r=[("cl", tc) for tc in range(8)], w=["cl_d"], sem="clst")
    return outk + ["cl_d"]


def fox_phase_b(c, ws, L, Win, Wout, ident_d, tri_d, mneg_d, KT_own, V_own, cl_own, KT_prev, V_prev, cl_prev):
    p = c.p
    hkeys = [("hT", kc) for kc in range(KC)]
    scale = 128.0 ** -0.5
    tri = p.sb("tri", [128, 128], F32)
    mneg = p.sb("mneg", [128, 128], F32)
    onef = p.sb("onef", [128, 128], F32)
    clq = p.sb("clq", [128, 4, 8, 16], F32)
    totr = p.sb("totr", [128, 3, 16], F32)
    Dr = p.sb("Dr", [128, 4, 16], F32)
    bcol = p.sb("bcol", [128, 4, 8, 16], F32)
    c.identf = p.sb("identf", [128, 128], F32)
    trh = p.sb("trh", [128, 128], F32)
    cqr = p.sb("cqr", [128, T], F32)
    qT = p.sb("fqT", [128, T], BF16)
    ktb = [p.sb(f"fktb{i}", [128, T], BF16) for i in range(2)]
    vtb = [p.sb(f"fvtb{i}", [128, 8, 128], BF16) for i in range(2)]
    tmpb = [p.sb(f"ftmp{i}", [128, 512], F32) for i in range(2)]
    ptb = [p.sb(f"fpt{i}", [128, 512], BF16) for i in range(2)]
    rl = p.sb("frl", [128, 512], F32)
    ob = p.sb("fob", [128, 512], F32)
    sgb = p.sb("fsg", [128, 512], F32)
    yT4 = p.sb("fy4", [128, 4, T], BF16)
    p.dma("sp", tri[:], tri_d, w=["tri"], sem="tri")
    p.dma("sp", mneg[:], mneg_d, w=["mneg"], sem="mneg")
    p.dma("sp", c.identf[:], ident_d, w=["identf"], sem="identf")
    p.memset("pool", onef[:], 1.0, w=["onef"])
    for r in range(3):
        p.dma("act", clq[:, r], cl_prev[r].rearrange("t p h -> p t h"), w=[("clq", r)], sem="clq")
        p.dma("act", totr[:, r, :], cl_prev[r, 7, 127:128, :].partition_broadcast(128), w=[("totr", r)], sem="totr")
    p.dma("act", clq[:, 3], cl_own.rearrange("t p h -> p t h"), w=[("clq", 3)], sem="clq")
    p.memset("dve", Dr[:, 3, :], 0.0, w=[("Dr", 3)])
    p.copy("dve", Dr[:, 2, :], totr[:, 2, :], r=[("totr", 2)], w=[("Dr", 2)])
    p.tt("dve", Dr[:, 1, :], Dr[:, 2, :], totr[:, 1, :], ALU.add, r=[("Dr", 2), ("totr", 1)], w=[("Dr", 1)])
    p.tt("dve", Dr[:, 0, :], Dr[:, 1, :], totr[:, 0, :], ALU.add, r=[("Dr", 1), ("totr", 0)], w=[("Dr", 0)])
    for r in range(3):
        p.ts("dve", Dr[:, r, :], Dr[:, r, :], c.prm[:, PRM_FXNEG + r:PRM_FXNEG + r + 1], None, ALU.add,
             r=[("Dr", r), "prm"], w=[("Dr", r)])
    for r in range(4):
        for kb in range(8):
            p.tt("dve", bcol[:, r, kb, :], Dr[:, r, :], clq[:, r, kb, :], ALU.subtract, r=[("Dr", r), ("clq", r)], w=["bcol"])
    li = 0
    for h in range(FX_H):
        hl = h % 4
        if hl == 0:
            wq, wqk, wqh = ws.get(*colpanel(Win, h * 128, 512))
            wg, wgk, wgh = ws.get(*colpanel(Win, 6144 + h * 128, 512))
        fox_qk(c, ws, Win, 0, PRM_QNG, h, (wq, wqk), qT, [("fqT", 0), ("fqT", 1)])
        for tc in range(8):
            p.ts("dve", trh[:], c.identf[:], clq[:, 3, tc, h:h + 1], None, ALU.mult, r=[("clq", 3), "identf"], w=["trh"])
            ps, psk = next_ps(c)
            p.mm(ps[:, 0:128], [(onef[:], trh[:])], r=["onef", "trh"], w=[psk])
            p.copy("act", cqr[:, tc * 128:(tc + 1) * 128], ps[:, 0:128], r=[psk], w=[("cqr", tc // 4)])
        for qh in range(2):
            q0 = qh * 512
            po, pok = c.ps[6], ("ps", 6)
            pl, plk = c.ps[7], ("ps", 7)
            blocks = [(r, kb) for r in range(3) for kb in range(8)] + [(3, kb) for kb in range(4 * (qh + 1))]
            cur = None
            for bi, (r, kb) in enumerate(blocks):
                if cur != r:
                    cur = r
                    kt_ = ktb[li % 2]
                    vt_ = vtb[li % 2]
                    ktk, vtk = ("fktb", li % 2), ("fvtb", li % 2)
                    li += 1
                    ksrc = KT_own[h] if r == 3 else KT_prev[r, h]
                    vsrc = (V_own if r == 3 else V_prev[r])[:, :, h * 128:(h + 1) * 128].rearrange("t p d -> p t d")
                    p.dma("sp", kt_[:], ksrc, w=[ktk], sem=("fkld", li % 2))
                    p.dma("act", vt_[:], vsrc, w=[vtk], sem=("fvld", li % 2))
                lo = max(kb * 128, q0) if r == 3 else q0
                w_ = q0 + 512 - lo
                ps, psk = c.ps[bi % 6], ("ps", bi % 6)
                p.mm(ps[:, 0:w_], [(kt_[:, kb * 128:(kb + 1) * 128], qT[:, lo:q0 + 512])], r=[ktk, ("fqT", qh)], w=[psk])
                tm = tmpb[bi % 2]
                tmk = ("ftmp", bi % 2)
                p.stt("dve", tm[:, 0:w_], ps[:, 0:w_], scale, cqr[:, lo:q0 + 512], ALU.mult, ALU.add, r=[psk, ("cqr", qh)], w=[tmk])
                if r == 3 and kb * 128 >= q0:
                    p.tt("dve", tm[:, 0:128], tm[:, 0:128], mneg[:], ALU.add, r=[tmk, "mneg"], w=[tmk])
                pt = ptb[bi % 2]
                ptk = ("fpt", bi % 2)
                p.act(pt[:, 0:w_], tm[:, 0:w_], AF.Exp, r=[tmk, "bcol"], w=[ptk], bias=bcol[:, r, kb, h:h + 1])
                first = bi == 0
                last = bi == len(blocks) - 1
                p.mm(po[:, lo - q0:512], [(vt_[:, kb, :], pt[:, 0:w_])], r=[vtk, ptk], w=[pok], start=first, stop=last)
                p.mm(pl[:, lo - q0:512], [(c.ones[:], pt[:, 0:w_])], r=["ones", ptk], w=[plk], start=first, stop=last)
            sl = slice(q0, q0 + 512)
            p.rec("dve", lambda e: e.reciprocal(out=rl[:], in_=pl[:]), r=[plk], w=["frl"])
            p.tt("dve", ob[:], po[:], rl[:], ALU.mult, r=[pok, "frl"], w=["fob"])
            pg, pgk = c.ps[(len(blocks)) % 6], ("ps", (len(blocks)) % 6)
            p.mm(pg[:], [(wg[:, kc, hl * 128:(hl + 1) * 128], c.hT[:, kc, sl]) for kc in range(KC)], r=[wgk] + hkeys, w=[pgk])
            p.act(sgb[:], pg[:], AF.Sigmoid, r=[pgk], w=["fsg"])
            p.tt("pool", yT4[:, hl, sl], ob[:], sgb[:], ALU.mult, r=["fob", "fsg"], w=[("fy4", hl, qh)])
        if hl == 3:
            ws.done(wqh)
            ws.done(wgh)
            wo, wok, woh = ws.get(*rowpanel(Wout, (h // 4) * 512, 4))
            for oc in range(KC):
                for half in range(2):
                    sl = slice(half * 512, (half + 1) * 512)
                    po2, pok2 = c.ps[(oc * 2 + half) % 6], ("ps", (oc * 2 + half) % 6)
                    p.mm(po2[:], [(wo[:, a, oc * 128:(oc + 1) * 128], yT4[:, a, sl]) for a in range(4)],
                         r=[wok] + [("fy4", a, half) for a in range(4)], w=[pok2])
                    p.tt("dve", c.xT[:, oc, sl], c.xT[:, oc, sl], po2[:], ALU.add, r=[pok2, ("xT", oc)], w=[("xT", oc)])
            ws.done(woh)


_PROGS = {}


def _finish(nc, body, nslot):
    pd = P(nc, dry=True)
    wsd = WStream(pd, nslot)
    body(pd, wsd)
    p = P(nc)
    ws = WStream(p, nslot, plan=wsd.plan)
    keys = body(p, ws)
    p.emit(keys)
    return nc


def build_prog(kind):
    if kind in _PROGS:
        return _PROGS[kind]
    nc = bass.Bass("TRN2", target_bir_lowering=False)
    dt = nc.dram_tensor
    x = dt("xT", [2048, 1024], F32, kind="ExternalInput").ap()
    prm = dt("prm", [128, PRM_COLS], F32, kind="ExternalInput").ap()
    if kind == "ffn":
        Wg = dt("wg", [2048, FFN_H], F32, kind="ExternalInput").ap()
        Wu = dt("wu", [2048, FFN_H], F32, kind="ExternalInput").ap()
        Wd = dt("wd", [FFN_H, 2048], F32, kind="ExternalInput").ap()
        y = dt("yT", [2048, 1024], F32, kind="ExternalOutput").ap()

        def body(p, ws):
            c = setup_common(p, prm)
            ffn_alloc(c)
            load_xT(c, x)
            ffn(c, ws, 0, Wg, Wu, Wd)
            return store_xT(c, y)
        nslot = 3
    elif kind in ("retA", "retB"):
        A = kind == "retA"
        io = "ExternalOutput" if A else "ExternalInput"
        Win = dt("w_in", [2048, 12288], F32, kind="ExternalInput").ap()
        kT_s = dt("kT_s", [128, 16, 1024], BF16, kind=io).ap()
        v_s = dt("v_s", [128, 8, 4096], BF16, kind=io).ap()
        cs = dt("cs", [2, 128, 1024], F32, kind=io).ap()
        if A:
            pos = dt("pos", [1, 1024], I32, kind="ExternalInput").ap()
            ident = dt("ident", [128, 128], F32, kind="ExternalInput").ap()
            S_loc = dt("S_loc", [8, 2, 128, 512], F32, kind="ExternalOutput").ap()
        else:
            S_all = dt("S_all", [3, 8, 2, 128, 512], F32, kind="ExternalInput").ap()
            Wout = dt("w_out", [4096, 2048], F32, kind="ExternalInput").ap()
            wtab = dt("wtab", [8, 128, 1024], F32, kind="ExternalInput").ap()
            qdec = dt("qdec", [8, 128, 1024], F32, kind="ExternalInput").ap()
            y = dt("yT", [2048, 1024], F32, kind="ExternalOutput").ap()

        def body(p, ws):
            c = setup_common(p, prm)
            ret_alloc(c)
            load_xT(c, x)
            if A:
                ret_tables(c, pos, cs)
                return ret_phase_a(c, ws, 0, 0, Win, pos, ident, kT_s, v_s, S_loc) + ["cs_d0", "cs_d1"]
            ret_tables_load(c, cs)
            rmsnorm(c, PRM_MIXG)
            ret_phase_b(c, ws, 0, 0, Win, Wout, wtab, qdec, kT_s, v_s, S_all)
            return store_xT(c, y)
        nslot = 2
    elif kind == "gmlp":
        Win = dt("w_in", [2048, 12288], F32, kind="ExternalInput").ap()
        Wout = dt("w_out", [6144, 2048], F32, kind="ExternalInput").ap()
        lng = dt("lng", [1, 6144], F32, kind="ExternalInput").ap()
        lnb = dt("lnb", [1, 6144], F32, kind="ExternalInput").ap()
        wsT = dt("wsT", [8, 128, 128], F32, kind="ExternalInput").ap()
        bs = dt("bs", [1, 1024], F32, kind="ExternalInput").ap()
        mask = dt("mask", [128, 128], F32, kind="ExternalInput").ap()
        vraw = dt("vraw", [8, 128, 6144], F32).ap()
        y = dt("yT", [2048, 1024], F32, kind="ExternalOutput").ap()

        def body(p, ws):
            c = setup_common(p, prm)
            load_xT(c, x)
            gmlp_layer(c, ws, 0, Win, Wout, lng, lnb, wsT, bs, mask, vraw)
            return store_xT(c, y)
        nslot = 3
    else:
        A = kind == "foxA"
        io = "ExternalOutput" if A else "ExternalInput"
        Win = dt("w_in", [2048, 8208], F32, kind="ExternalInput").ap()
        tri = dt("tri", [128, 128], F32, kind="ExternalInput").ap()
        KT = dt("KT", [16, 128, 1024], BF16, kind=io).ap()
        V = dt("V", [8, 128, 2048], BF16, kind=io).ap()
        cl = dt("cl", [8, 128, 16], F32, kind=io).ap()
        if A:
            bf = dt("bf", [1, 16], F32, kind="ExternalInput").ap()
        else:
            Wout = dt("w_out", [2048, 2048], F32, kind="ExternalInput").ap()
            ident = dt("ident", [128, 128], F32, kind="ExternalInput").ap()
            mneg = dt("mneg", [128, 128], F32, kind="ExternalInput").ap()
            KTp = dt("KTp", [3, 16, 128, 1024], BF16, kind="ExternalInput").ap()
            Vp = dt("Vp", [3, 8, 128, 2048], BF16, kind="ExternalInput").ap()
            clp = dt("clp", [3, 8, 128, 16], F32, kind="ExternalInput").ap()
            y = dt("yT", [2048, 1024], F32, kind="ExternalOutput").ap()

        def body(p, ws):
            c = setup_common(p, prm)
            fox_alloc(c)
            load_xT(c, x)
            if A:
                return fox_phase_a(c, ws, 0, Win, bf, tri, KT, V, cl)
            rmsnorm(c, PRM_MIXG)
            fox_phase_b(c, ws, 0, Win, Wout, ident, tri, mneg, KT, V, cl, KTp, Vp, clp)
            return store_xT(c, y)
        nslot = 3
    _finish(nc, body, nslot)
    _PROGS[kind] = nc
    return nc


def _run(kind, maps):
    nc = build_prog(kind)
    res = run_bass_kernel_spmd(nc, maps, core_ids=list(range(NCORES)))
    return res.results


def kernel(**inputs):
    inp = {k: np.asarray(v) for k, v in inputs.items()}
    x = inp["x"]
    cst = make_consts()
    tri = np.triu(np.ones((128, 128), np.float32))
    mneg = np.where(np.arange(128)[:, None] <= np.arange(128)[None, :], 0.0, NEG).astype(np.float32)
    xT = [np.ascontiguousarray(x[cid // 4, (cid % 4) * 1024:(cid % 4 + 1) * 1024].T) for cid in range(NCORES)]
    pos = [np.ascontiguousarray(inp["positions"][cid // 4, (cid % 4) * 1024:(cid % 4 + 1) * 1024].reshape(1, 1024).astype(np.int32))
           for cid in range(NCORES)]

    def ffn_launch(L):
        prm = [make_prm(inp, cid, L) for cid in range(NCORES)]
        maps = [{"xT": xT[cid], "prm": prm[cid], "wg": inp["ffn_w_gate"][L], "wu": inp["ffn_w_up"][L], "wd": inp["ffn_w_down"][L]}
                for cid in range(NCORES)]
        r = _run("ffn", maps)
        for cid in range(NCORES):
            xT[cid] = r[cid]["yT"]

    def ret_launch(L, j):
        prm = [make_prm(inp, cid, L, j) for cid in range(NCORES)]
        maps = [{"xT": xT[cid], "prm": prm[cid], "pos": pos[cid], "w_in": inp["ret_w_in"][j], "ident": cst["ident"]}
                for cid in range(NCORES)]
        ra = _run("retA", maps)
        maps = []
        for cid in range(NCORES):
            b = cid // 4
            S_all = np.stack([ra[b * 4 + s]["S_loc"] for s in range(3)])
            maps.append({"xT": xT[cid], "prm": prm[cid], "w_in": inp["ret_w_in"][j], "w_out": inp["ret_w_out"][j],
                         "wtab": cst["wtab"], "qdec": cst["qdec"], "kT_s": ra[cid]["kT_s"], "v_s": ra[cid]["v_s"],
                         "cs": ra[cid]["cs"], "S_all": S_all})
        rb = _run("retB", maps)
        for cid in range(NCORES):
            xT[cid] = rb[cid]["yT"]

    def gmlp_launch(L):
        prm = [make_prm(inp, cid, L) for cid in range(NCORES)]
        wsT = np.ascontiguousarray(inp["gmlp_w_s"][0].transpose(0, 2, 1))
        maps = [{"xT": xT[cid], "prm": prm[cid], "w_in": inp["gmlp_w_in"][0], "w_out": inp["gmlp_w_out"][0],
                 "lng": inp["gmlp_ln_g"][0].reshape(1, -1), "lnb": inp["gmlp_ln_b"][0].reshape(1, -1), "wsT": wsT,
                 "bs": inp["gmlp_b_s"][0].reshape(1, -1), "mask": tri} for cid in range(NCORES)]
        r = _run("gmlp", maps)
        for cid in range(NCORES):
            xT[cid] = r[cid]["yT"]

    def fox_launch(L):
        prm = [make_prm(inp, cid, L) for cid in range(NCORES)]
        maps = [{"xT": xT[cid], "prm": prm[cid], "w_in": inp["fox_w_in"][0], "tri": tri, "bf": inp["fox_b_f"][0].reshape(1, 16)}
                for cid in range(NCORES)]
        ra = _run("foxA", maps)
        maps = []
        for cid in range(NCORES):
            b, seg = cid // 4, cid % 4
            KTp = np.zeros((3,) + ra[cid]["KT"].shape, ra[cid]["KT"].dtype)
            Vp = np.zeros((3,) + ra[cid]["V"].shape, ra[cid]["V"].dtype)
            clp = np.zeros((3, 8, 128, 16), np.float32)
            for r_ in range(3):
                s_ = seg - 3 + r_
                if s_ >= 0:
                    KTp[r_] = ra[b * 4 + s_]["KT"]
                    Vp[r_] = ra[b * 4 + s_]["V"]
                    clp[r_] = ra[b * 4 + s_]["cl"]
            maps.append({"xT": xT[cid], "prm": prm[cid], "w_in": inp["fox_w_in"][0], "tri": tri, "KT": ra[cid]["KT"], "V": ra[cid]["V"],
                         "cl": ra[cid]["cl"], "w_out": inp["fox_w_out"][0], "ident": cst["ident"], "mneg": mneg,
                         "KTp": KTp, "Vp": Vp, "clp": clp})
        rb = _run("foxB", maps)
        for cid in range(NCORES):
            xT[cid] = rb[cid]["yT"]

    ret_launch(0, 0)
    ffn_launch(0)
    gmlp_launch(1)
    ffn_launch(1)
    fox_launch(2)
    ffn_launch(2)
    ret_launch(3, 1)
    ffn_launch(3)
    out = np.empty_like(x)
    for cid in range(NCORES):
        out[cid // 4, (cid % 4) * 1024:(cid % 4 + 1) * 1024] = np.asarray(xT[cid]).T
    return out
```
